# Optimizing a Trainium2 kernel written in Bass

```python
import math
import jax, jax.numpy as jnp
from jax import lax
import numpy as np

D_MODEL = 1024
BATCH = 4
SEQ = 8192
DEPTH = 2

CONV_WIDTH = 512
CONV_K = 3
N_HEADS = 16
HEAD_DIM = 64
N_KV_GROUPS = 4
HEADS_PER_GROUP = N_HEADS // N_KV_GROUPS
ATTN_WIDTH = N_HEADS * HEAD_DIM
KV_WIDTH = N_KV_GROUPS * HEAD_DIM
CMP_BLOCK = 32
CMP_STRIDE = 16
CMP_RATIO = CMP_BLOCK // CMP_STRIDE
CMP_HIDDEN = 128
SEL_BLOCK = 64
SEL_TOPN = 16
WINDOW = 512
Q_BLOCK = 128
D_FF = -(-8 * D_MODEL // (3 * 256)) * 256
ALPHA = (2.0 * DEPTH) ** 0.25
BETA = (8.0 * DEPTH) ** -0.25
LN_EPS = 1e-5
NEG_INF = -1e30
FORCE_BONUS = 1e4
IN_SIZES = [CONV_WIDTH] * 3 + [ATTN_WIDTH] + [KV_WIDTH] * 6 + [3 * N_HEADS, 2 * D_MODEL]
N_IN = sum(IN_SIZES)

kernel_name = "hybrid_shortconv_nsa_alibi_deepnorm"


def layer_norm(x, g, b):
    xf = x.astype(jnp.float32)
    mu = jnp.mean(xf, axis=-1, keepdims=True)
    var = jnp.mean(jnp.square(xf - mu), axis=-1, keepdims=True)
    y = (xf - mu) * lax.rsqrt(var + LN_EPS)
    return (y * g.astype(jnp.float32) + b.astype(jnp.float32)).astype(x.dtype)


def alibi_slopes():
    h = jnp.arange(1, N_HEADS + 1, dtype=jnp.float32)
    return (2.0 ** (-8.0 * h / N_HEADS)).reshape(N_KV_GROUPS, HEADS_PER_GROUP)


def masked_softmax(s, mask):
    p = jax.nn.softmax(jnp.where(mask, s, NEG_INF), axis=-1)
    return jnp.where(mask, p, 0.0)


def short_conv_mixer(b_gate, c_gate, xin, conv_w, w_out):
    u = c_gate * xin
    v = lax.conv_general_dilated(
        u, conv_w[:, None, :], window_strides=(1,), padding=((CONV_K - 1, 0),),
        dimension_numbers=("NWC", "WIO", "NWC"), feature_group_count=CONV_WIDTH)
    return (b_gate * v) @ w_out


def compress_blocks(kv, pos, w1, b1, w2):
    B_, G_, S_, dk = kv.shape
    n_chunks = S_ // CMP_STRIDE
    n_c = n_chunks - CMP_RATIO + 1
    chunks = kv.reshape(B_, G_, n_chunks, CMP_STRIDE, dk)
    blocks = jnp.concatenate([chunks[:, :, r:r + n_c] for r in range(CMP_RATIO)], axis=3)
    flat = (blocks + pos).reshape(B_, G_, n_c, CMP_BLOCK * dk)
    return jax.nn.gelu(flat @ w1 + b1) @ w2


def nsa_attention(q, k_cmp, v_cmp, k_slc, v_slc, k_win, v_win, gates, slopes):
    B_, G_, Hg, S_, dk = q.shape
    n_c = k_cmp.shape[2]
    n_sel = S_ // SEL_BLOCK
    topn = min(SEL_TOPN, n_sel)
    scale = HEAD_DIM ** -0.5
    slopes_ = slopes[None, :, :, None, None]
    cmp_end = jnp.arange(n_c) * CMP_STRIDE + CMP_BLOCK - 1
    ci = jnp.arange(n_c)[:, None]
    sj = jnp.arange(n_sel)[None, :]
    overlap = ((ci * CMP_STRIDE <= sj * SEL_BLOCK + SEL_BLOCK - 1)
               & (ci * CMP_STRIDE + CMP_BLOCK - 1 >= sj * SEL_BLOCK)).astype(jnp.float32)
    k_blocks = k_slc.reshape(B_, G_, n_sel, SEL_BLOCK, dk)
    v_blocks = v_slc.reshape(B_, G_, n_sel, SEL_BLOCK, dk)
    k_win_p = jnp.pad(k_win, ((0, 0), (0, 0), (WINDOW, 0), (0, 0)))
    v_win_p = jnp.pad(v_win, ((0, 0), (0, 0), (WINDOW, 0), (0, 0)))
    b_idx = jnp.arange(B_)[:, None, None, None]
    g_idx = jnp.arange(G_)[None, :, None, None]
    jj = jnp.arange(n_sel)
    sel_off = jnp.arange(SEL_BLOCK)
    win_off = jnp.arange(Q_BLOCK + WINDOW)

    def query_block(qb):
        q0 = qb * Q_BLOCK
        t = q0 + jnp.arange(Q_BLOCK)
        qc = lax.dynamic_slice_in_dim(q, q0, Q_BLOCK, axis=3) * scale
        gc = lax.dynamic_slice_in_dim(gates, q0, Q_BLOCK, axis=3)

        dist_c = t[:, None] - cmp_end[None, :]
        s_c = jnp.einsum("bghqd,bgcd->bghqc", qc, k_cmp).astype(jnp.float32)
        s_c = s_c - slopes_ * dist_c.astype(jnp.float32)
        p_c = masked_softmax(s_c, dist_c >= 0)
        o_c = jnp.einsum("bghqc,bgcd->bghqd", p_c.astype(v_cmp.dtype), v_cmp)

        imp = jnp.einsum("bgqc,cj->bgqj", p_c.sum(axis=2), overlap)
        cur = t // SEL_BLOCK
        forced = (jj[None, :] == 0) | (jj[None, :] == cur[:, None]) | (jj[None, :] == cur[:, None] - 1)
        imp = jnp.where(forced, imp + FORCE_BONUS, imp)
        imp = jnp.where(jj[None, :] <= cur[:, None], imp, NEG_INF)
        top_val, top_idx = lax.top_k(imp, topn)
        blk_ok = top_val > 0.5 * NEG_INF

        k_g = k_blocks[b_idx, g_idx, top_idx]
        v_g = v_blocks[b_idx, g_idx, top_idx]
        pos_s = top_idx[..., None] * SEL_BLOCK + sel_off
        dist_s = t[None, None, :, None, None] - pos_s
        mask_s = (dist_s >= 0) & blk_ok[..., None]
        s_s = jnp.einsum("bghqd,bgqnld->bghqnl", qc, k_g).astype(jnp.float32)
        s_s = s_s - slopes_[..., None] * dist_s[:, :, None].astype(jnp.float32)
        p_s = masked_softmax(s_s.reshape(B_, G_, Hg, Q_BLOCK, topn * SEL_BLOCK),
                             mask_s[:, :, None].reshape(B_, G_, 1, Q_BLOCK, topn * SEL_BLOCK))
        o_s = jnp.einsum("bghqm,bgqmd->bghqd", p_s.astype(v_g.dtype),
                         v_g.reshape(B_, G_, Q_BLOCK, topn * SEL_BLOCK, dk))

        kw = lax.dynamic_slice_in_dim(k_win_p, q0, Q_BLOCK + WINDOW, axis=2)
        vw = lax.dynamic_slice_in_dim(v_win_p, q0, Q_BLOCK + WINDOW, axis=2)
        pos_w = q0 - WINDOW + win_off
        dist_w = t[:, None] - pos_w[None, :]
        mask_w = (dist_w >= 0) & (dist_w < WINDOW) & (pos_w[None, :] >= 0)
        s_w = jnp.einsum("bghqd,bgkd->bghqk", qc, kw).astype(jnp.float32)
        s_w = s_w - slopes_ * dist_w.astype(jnp.float32)
        p_w = masked_softmax(s_w, mask_w)
        o_w = jnp.einsum("bghqk,bgkd->bghqd", p_w.astype(vw.dtype), vw)

        return gc[..., 0:1] * o_c + gc[..., 1:2] * o_s + gc[..., 2:3] * o_w

    out = lax.map(query_block, jnp.arange(S_ // Q_BLOCK))
    return out.transpose(1, 0, 4, 2, 3, 5).reshape(B_, S_, N_HEADS * HEAD_DIM)


def hybrid_mixer(x, w_in, conv_w, w_conv_out, cmp_pos, cmp_w1, cmp_b1, cmp_w2, w_o, slopes):
    B_, S_, _ = x.shape
    proj = x @ w_in
    offsets = []
    acc = 0
    for sz in IN_SIZES[:-1]:
        acc += sz
        offsets.append(acc)
    (b_g, c_g, xin, q, kc, vc, ks, vs, kw, vw, nsa_g, merge_g) = jnp.split(proj, offsets, axis=-1)

    conv_out = short_conv_mixer(b_g, c_g, xin, conv_w, w_conv_out)

    def to_groups(t):
        return t.reshape(B_, S_, N_KV_GROUPS, HEAD_DIM).transpose(0, 2, 1, 3)

    qh = q.reshape(B_, S_, N_KV_GROUPS, HEADS_PER_GROUP, HEAD_DIM).transpose(0, 2, 3, 1, 4)
    k_cmp = compress_blocks(to_groups(kc), cmp_pos[0], cmp_w1[0], cmp_b1[0], cmp_w2[0])
    v_cmp = compress_blocks(to_groups(vc), cmp_pos[1], cmp_w1[1], cmp_b1[1], cmp_w2[1])
    gates = jax.nn.sigmoid(nsa_g).reshape(B_, S_, N_KV_GROUPS, HEADS_PER_GROUP, 3).transpose(0, 2, 3, 1, 4)
    attn_out = nsa_attention(qh, k_cmp, v_cmp, to_groups(ks), to_groups(vs),
                             to_groups(kw), to_groups(vw), gates, slopes)

    g_conv, g_attn = jnp.split(jax.nn.sigmoid(merge_g), 2, axis=-1)
    return (g_conv * conv_out + g_attn * attn_out) @ w_o


def swiglu(x, w_ffn_in, w_ffn_out):
    a, u = jnp.split(x @ w_ffn_in, 2, axis=-1)
    return (jax.nn.silu(a) * u) @ w_ffn_out


def setup_inputs(seed: int = 0) -> dict:
    key = jax.random.key(seed)
    ks = jax.random.split(key, 15)

    def nrm(k, shape, scale):
        return scale * jax.random.normal(k, shape, jnp.float32)

    return {
        "x": nrm(ks[0], (BATCH, SEQ, D_MODEL), 1.0),
        "w_in": nrm(ks[1], (DEPTH, D_MODEL, N_IN), D_MODEL ** -0.5),
        "conv_w": nrm(ks[2], (DEPTH, CONV_K, CONV_WIDTH), CONV_K ** -0.5),
        "w_conv_out": nrm(ks[3], (DEPTH, CONV_WIDTH, D_MODEL), CONV_WIDTH ** -0.5),
        "cmp_pos": nrm(ks[4], (DEPTH, 2, CMP_BLOCK, HEAD_DIM), 0.02),
        "cmp_w1": nrm(ks[5], (DEPTH, 2, CMP_BLOCK * HEAD_DIM, CMP_HIDDEN), (CMP_BLOCK * HEAD_DIM) ** -0.5),
        "cmp_b1": nrm(ks[6], (DEPTH, 2, CMP_HIDDEN), 0.01),
        "cmp_w2": nrm(ks[7], (DEPTH, 2, CMP_HIDDEN, HEAD_DIM), CMP_HIDDEN ** -0.5),
        "w_o": nrm(ks[8], (DEPTH, D_MODEL, D_MODEL), BETA * D_MODEL ** -0.5),
        "ln1_g": 1.0 + nrm(ks[9], (DEPTH, D_MODEL), 0.02),
        "ln1_b": nrm(ks[10], (DEPTH, D_MODEL), 0.01),
        "w_ffn_in": nrm(ks[11], (DEPTH, D_MODEL, 2 * D_FF), D_MODEL ** -0.5),
        "w_ffn_out": nrm(ks[12], (DEPTH, D_FF, D_MODEL), BETA * D_FF ** -0.5),
        "ln2_g": 1.0 + nrm(ks[13], (DEPTH, D_MODEL), 0.02),
        "ln2_b": nrm(ks[14], (DEPTH, D_MODEL), 0.01),
    }


def reference(x, w_in, conv_w, w_conv_out, cmp_pos, cmp_w1, cmp_b1, cmp_w2, w_o,
              ln1_g, ln1_b, w_ffn_in, w_ffn_out, ln2_g, ln2_b):
    slopes = alibi_slopes()
    for l in range(DEPTH):
        mix = hybrid_mixer(x, w_in[l], conv_w[l], w_conv_out[l], cmp_pos[l], cmp_w1[l],
                           cmp_b1[l], cmp_w2[l], w_o[l], slopes)
        x = layer_norm(ALPHA * x + mix, ln1_g[l], ln1_b[l])
        x = layer_norm(ALPHA * x + swiglu(x, w_ffn_in[l], w_ffn_out[l]), ln2_g[l], ln2_b[l])
    return x
```

```python
import numpy as np
import ml_dtypes
from contextlib import ExitStack
import concourse.bass as bass
import concourse.mybir as mybir
from concourse.bass_utils import run_bass_kernel_spmd

F32 = mybir.dt.float32
BF16 = mybir.dt.bfloat16
ALU = mybir.AluOpType
AF = mybir.ActivationFunctionType
NPBF = ml_dtypes.bfloat16

D = 1024
NIN = 6192
DFF = 2816
NH = 16
DEPTH = 2
ALPHA = (2.0 * DEPTH) ** 0.25
LN_EPS = 1e-5
NEGM = 29952.0
NCORES = 8


class TT:
    def __init__(self, ap):
        self.ap = ap
        self.w = None
        self.r = {}

    def __getitem__(self, k):
        return self.ap[k]


class KB:
    def __init__(self, nc, ndsem=40):
        self.nc = nc
        self.es = ExitStack()
        self.E = {"pe": nc.tensor, "act": nc.scalar, "dve": nc.vector, "pool": nc.gpsimd, "sp": nc.sync}
        self.sem = {k: self.es.enter_context(nc.semaphore("s_" + k)) for k in ("pe", "act", "dve", "pool")}
        self.cnt = {k: 0 for k in self.sem}
        self.dsem = [self.es.enter_context(nc.semaphore("d%d" % i)) for i in range(ndsem)]
        self.dval = [0] * ndsem
        self.dnext = 0
        self.seen = {k: {} for k in self.E}
        self.ntile = 0

    def tile(self, shape, dt, name=None):
        self.ntile += 1
        return TT(self.es.enter_context(self.nc.sbuf_tensor(name or ("t%d" % self.ntile), list(shape), dt)))

    def psum(self, shape, dt, name=None):
        self.ntile += 1
        return TT(self.es.enter_context(self.nc.psum_tensor(name or ("p%d" % self.ntile), list(shape), dt)))

    def _wait(self, e, deps):
        best = {}
        for d in deps:
            if d is None:
                continue
            key = (d[0], d[1])
            if best.get(key, 0) < d[2]:
                best[key] = d[2]
        eng = self.E[e]
        seen = self.seen[e]
        for (kind, a), n in best.items():
            if kind == "c":
                if a == e and e == "pe":
                    continue
                if seen.get(a, 0) >= n:
                    continue
                eng.wait_ge(self.sem[a], n)
                seen[a] = n
            else:
                if seen.get(("d", a), 0) >= n:
                    continue
                eng.wait_ge(self.dsem[a], n)
                seen[("d", a)] = n

    @staticmethod
    def _deps(r, w):
        deps = []
        for t in r:
            if t.w is not None:
                deps.append(t.w)
        for t in w:
            if t.w is not None:
                deps.append(t.w)
            deps.extend(t.r.values())
        return deps

    def op(self, e, fn, r=(), w=()):
        self._wait(e, self._deps(r, w))
        ins = fn(self.E[e])
        self.cnt[e] += 1
        ins.then_inc(self.sem[e], 1)
        tok = ("c", e, self.cnt[e])
        for t in r:
            t.r[("c", e)] = tok
        for t in w:
            t.w = tok
            t.r = {}
        return tok

    def dma(self, q, out, in_, r=(), w=()):
        deps = self._deps(r, w)
        k = self.dnext
        self.dnext = (k + 1) % len(self.dsem)
        if self.dval[k]:
            deps.append(("d", k, self.dval[k]))
        self._wait(q, deps)
        self.dval[k] += 16
        self.E[q].dma_start(out=out, in_=in_).then_inc(self.dsem[k], 16)
        tok = ("d", k, self.dval[k])
        for t in r:
            t.r[("d", k)] = tok
        for t in w:
            t.w = tok
            t.r = {}
        return tok

    def finish(self):
        for k, v in enumerate(self.dval):
            if v:
                self._wait("sp", [("d", k, v)])
        self.es.close()


class Rot:
    def __init__(self, items):
        self.items = items
        self.i = 0

    def __call__(self):
        t = self.items[self.i % len(self.items)]
        self.i += 1
        return t


def build_W(ncol):
    nc = bass.Bass("TRN2", target_bir_lowering=False)
    src = nc.dram_tensor("src", [128, ncol], F32, kind="ExternalInput").ap()
    dst = nc.dram_tensor("dst", [128, ncol], BF16, kind="ExternalOutput").ap()
    kb = KB(nc)
    CH = 2048
    sin = Rot([kb.tile([128, CH], F32) for _ in range(3)])
    sout = Rot([kb.tile([128, CH], BF16) for _ in range(3)])
    i = 0
    c0 = 0
    while c0 < ncol:
        n = min(CH, ncol - c0)
        a = sin()
        b = sout()
        kb.dma("sp", a[:, 0:n], src[:, c0:c0 + n], w=[a])
        if i % 2 == 0:
            kb.op("dve", lambda e: e.tensor_copy(out=b[:, 0:n], in_=a[:, 0:n]), r=[a], w=[b])
        else:
            kb.op("act", lambda e: e.activation(out=b[:, 0:n], in_=a[:, 0:n], func=AF.Copy), r=[a], w=[b])
        kb.dma("pool", dst[:, c0:c0 + n], b[:, 0:n], r=[b])
        c0 += n
        i += 1
    kb.finish()
    return nc


def build_A(S):
    TOK = S // 2
    NST = TOK // 512
    nc = bass.Bass("TRN2", target_bir_lowering=False)
    xT = nc.dram_tensor("xT", [8, 128, TOK + 2], F32, kind="ExternalInput").ap()
    w_in = nc.dram_tensor("w_in", [8, 128, NIN], BF16, kind="ExternalInput").ap()
    wco_d = nc.dram_tensor("wco", [4, 128, D], BF16, kind="ExternalInput").ap()
    cw_d = nc.dram_tensor("convw", [128, 12], F32, kind="ExternalInput").ap()
    qT = nc.dram_tensor("qT", [D, TOK], BF16, kind="ExternalOutput").ap()
    kT = nc.dram_tensor("kT", [4, 256, TOK], BF16, kind="ExternalOutput").ap()
    vtok = nc.dram_tensor("vtok", [TOK, 512], BF16, kind="ExternalOutput").ap()
    gates = nc.dram_tensor("gates", [TOK, 48], F32, kind="ExternalOutput").ap()
    cmT = nc.dram_tensor("cmT", [D, TOK], F32, kind="ExternalOutput").ap()
    gaT = nc.dram_tensor("gaT", [D, TOK], F32, kind="ExternalOutput").ap()
    kb = KB(nc)

    wb = [kb.tile([128, NIN], BF16) for _ in range(8)]
    for k in range(8):
        kb.dma("sp", wb[k][:, :], w_in[k], w=[wb[k]])
    wco = [kb.tile([128, D], BF16) for _ in range(4)]
    for j in range(4):
        kb.dma("sp", wco[j][:, :], wco_d[j], w=[wco[j]])
    cw = kb.tile([128, 12], F32)
    kb.dma("sp", cw[:, :], cw_d[:, :], w=[cw])

    xs = [kb.tile([128, 514], F32) for _ in range(8)]
    xb = [[kb.tile([128, 514], BF16) for _ in range(8)] for _ in range(2)]
    u = [kb.tile([128, 514], F32) for _ in range(4)]
    v = kb.tile([128, 512], F32)
    bvb = [kb.tile([128, 512], BF16) for _ in range(4)]
    csb = kb.tile([128, 512], F32)
    chal = kb.tile([128, 2], F32)
    f32r = Rot([kb.tile([128, 512], F32) for _ in range(6)])
    bfr = Rot([kb.tile([128, 512], BF16) for _ in range(6)])
    gor = Rot([kb.tile([128, 48], F32) for _ in range(2)])
    PS = Rot([kb.psum([128, 512], F32) for _ in range(8)])

    def load_x(st):
        for k in range(8):
            kb.dma("sp", xs[k][:, :], xT[k, :, st * 512: st * 512 + 514], w=[xs[k]])
            xk = xb[st % 2][k]
            kb.op("pool", lambda e, k=k, xk=xk: e.tensor_copy(out=xk[:, :], in_=xs[k][:, :]), r=[xs[k]], w=[xk])

    load_x(0)
    for st in range(NST):
        xbt = xb[st % 2]
        if st + 1 < NST:
            load_x(st + 1)
        t0 = st * 512

        def mm_group(ps, c0, M, lo=2, n=512):
            for k in range(8):
                kb.op("pe", lambda e, k=k: e.matmul(ps[0:M, 0:n], wb[k][:, c0:c0 + M], xbt[k][:, lo:lo + n],
                                                    start=(k == 0), stop=(k == 7)),
                      r=[wb[k], xbt[k]], w=[ps])

        for j in range(4):
            if st == 0:
                pch = PS()
                mm_group(pch, 512 + 128 * j, 128, lo=0, n=2)
                kb.op("act", lambda e: e.activation(out=chal[:, :], in_=pch[:, 0:2], func=AF.Copy), r=[pch], w=[chal])
                pxh = PS()
                mm_group(pxh, 1024 + 128 * j, 128, lo=0, n=2)
                kb.op("dve", lambda e: e.tensor_tensor(out=u[j][:, 0:2], in0=chal[:, :], in1=pxh[:, 0:2], op=ALU.mult),
                      r=[chal, pxh], w=[u[j]])
            else:
                kb.op("dve", lambda e: e.tensor_copy(out=u[j][:, 0:2], in_=u[j][:, 512:514]), r=[u[j]], w=[u[j]])
            pc = PS()
            mm_group(pc, 512 + 128 * j, 128)
            kb.op("act", lambda e: e.activation(out=csb[:, :], in_=pc[:, :], func=AF.Copy), r=[pc], w=[csb])
            px = PS()
            mm_group(px, 1024 + 128 * j, 128)
            kb.op("dve", lambda e: e.tensor_tensor(out=u[j][:, 2:514], in0=csb[:, :], in1=px[:, :], op=ALU.mult),
                  r=[csb, px], w=[u[j]])
            kb.op("dve", lambda e: e.tensor_scalar(out=v[:, :], in0=u[j][:, 0:512], scalar1=cw[:, 3 * j:3 * j + 1],
                                                   scalar2=None, op0=ALU.mult), r=[u[j], cw], w=[v])
            kb.op("dve", lambda e: e.scalar_tensor_tensor(out=v[:, :], in0=u[j][:, 1:513], scalar=cw[:, 3 * j + 1:3 * j + 2],
                                                          in1=v[:, :], op0=ALU.mult, op1=ALU.add), r=[u[j], cw, v], w=[v])
            kb.op("dve", lambda e: e.scalar_tensor_tensor(out=v[:, :], in0=u[j][:, 2:514], scalar=cw[:, 3 * j + 2:3 * j + 3],
                                                          in1=v[:, :], op0=ALU.mult, op1=ALU.add), r=[u[j], cw, v], w=[v])
            pb = PS()
            mm_group(pb, 128 * j, 128)
            kb.op("dve", lambda e: e.tensor_tensor(out=bvb[j][:, :], in0=v[:, :], in1=pb[:, :], op=ALU.mult),
                  r=[v, pb], w=[bvb[j]])
        for i in range(8):
            pco = PS()
            for j in range(4):
                kb.op("pe", lambda e, j=j: e.matmul(pco[:, :], wco[j][:, 128 * i:128 * i + 128], bvb[j][:, :],
                                                    start=(j == 0), stop=(j == 3)), r=[wco[j], bvb[j]], w=[pco])
            pg = PS()
            mm_group(pg, 4144 + 128 * i, 128)
            gs = f32r()
            kb.op("act", lambda e: e.activation(out=gs[:, :], in_=pg[:, :], func=AF.Sigmoid), r=[pg], w=[gs])
            cmo = f32r()
            kb.op("dve", lambda e: e.tensor_tensor(out=cmo[:, :], in0=gs[:, :], in1=pco[:, :], op=ALU.mult),
                  r=[gs, pco], w=[cmo])
            kb.dma("pool", cmT[128 * i:128 * i + 128, t0:t0 + 512], cmo[:, :], r=[cmo])
            pg2 = PS()
            mm_group(pg2, 5168 + 128 * i, 128)
            gao = f32r()
            kb.op("act", lambda e: e.activation(out=gao[:, :], in_=pg2[:, :], func=AF.Sigmoid), r=[pg2], w=[gao])
            kb.dma("pool", gaT[128 * i:128 * i + 128, t0:t0 + 512], gao[:, :], r=[gao])
        for i in range(8):
            p = PS()
            mm_group(p, 1536 + 128 * i, 128)
            qo = bfr()
            kb.op("act", lambda e: e.activation(out=qo[:, :], in_=p[:, :], func=AF.Copy, scale=0.125), r=[p], w=[qo])
            kb.dma("pool", qT[128 * i:128 * i + 128, t0:t0 + 512], qo[:, :], r=[qo])
        for ty, base in enumerate((3072, 3584, 2560, 2816)):
            for c in range(2):
                p = PS()
                mm_group(p, base + 128 * c, 128)
                ko = bfr()
                kb.op("dve", lambda e: e.tensor_copy(out=ko[:, :], in_=p[:, :]), r=[p], w=[ko])
                kb.dma("pool", kT[ty, 128 * c:128 * c + 128, t0:t0 + 512], ko[:, :], r=[ko])
        for tt in range(4):
            lo = 2 + 128 * tt
            pv = PS()
            for (cb, off) in ((3328, 0), (3840, 256)):
                for k in range(8):
                    kb.op("pe", lambda e, k=k: e.matmul(pv[:, off:off + 256], xbt[k][:, lo:lo + 128], wb[k][:, cb:cb + 256],
                                                        start=(k == 0), stop=(k == 7)), r=[wb[k], xbt[k]], w=[pv])
            vo = bfr()
            kb.op("act", lambda e: e.activation(out=vo[:, :], in_=pv[:, :], func=AF.Copy), r=[pv], w=[vo])
            kb.dma("pool", vtok[t0 + 128 * tt:t0 + 128 * tt + 128, :], vo[:, :], r=[vo])
            pgt = PS()
            for k in range(8):
                kb.op("pe", lambda e, k=k: e.matmul(pgt[:, 0:48], xbt[k][:, lo:lo + 128], wb[k][:, 4096:4144],
                                                    start=(k == 0), stop=(k == 7)), r=[wb[k], xbt[k]], w=[pgt])
            go = gor()
            kb.op("act", lambda e: e.activation(out=go[:, :], in_=pgt[:, 0:48], func=AF.Sigmoid), r=[pgt], w=[go])
            kb.dma("pool", gates[t0 + 128 * tt:t0 + 128 * tt + 128, :], go[:, :], r=[go])
    kb.finish()
    return nc


def build_C(S):
    NQT = S // 128
    NKT = S // 128
    n_c = S // 16 - 1
    NCC = (n_c + 127) // 128
    nc = bass.Bass("TRN2", target_bir_lowering=False)
    dt_in = lambda name, shape, dt: nc.dram_tensor(name, list(shape), dt, kind="ExternalInput").ap()
    qT = dt_in("qT", [8, 64, S], BF16)
    qaug = dt_in("qaug", [8, 8, S], BF16)
    ksT = dt_in("ksT", [2, 64, S], BF16)
    kwT = dt_in("kwT", [2, 64, S], BF16)
    kcT = dt_in("kcT", [2, 2, 64, S], BF16)
    kaug = dt_in("kaug", [8, S], BF16)
    caug = dt_in("caug", [8, NCC * 128], BF16)
    vs = dt_in("vs", [2, S, 64], BF16)
    vw = dt_in("vw", [2, S, 64], BF16)
    gates = dt_in("gates", [S, 24], F32)
    w1 = dt_in("w1", [2, 64, 32 * 128], BF16)
    w2 = dt_in("w2", [2, 128, 64], BF16)
    b1 = dt_in("b1", [128, 2], F32)
    posT = dt_in("posT", [2, 64, 32], F32)
    ident_d = dt_in("ident", [128, 128], BF16)
    Ex_d = dt_in("Ex", [128, NKT * 128], BF16)
    triC_d = dt_in("triC", [128, 512], BF16)
    triW_d = dt_in("triW", [128, 512], BF16)
    cmask_d = dt_in("cmask", [17, 128, 512], BF16)
    F_d = dt_in("Ftab", [NQT, 128, 128], F32)
    ovl_d = dt_in("ovl", [NCC, 128, 129], BF16)
    attnT = nc.dram_tensor("attnT", [512, S], BF16, kind="ExternalOutput").ap()
    kb = KB(nc)

    ident = kb.tile([128, 128], BF16)
    kb.dma("sp", ident[:, :], ident_d[:, :], w=[ident])
    Ex = kb.tile([128, NKT * 128], BF16)
    kb.dma("sp", Ex[:, :], Ex_d[:, :], w=[Ex])
    triC = kb.tile([128, 512], BF16)
    kb.dma("sp", triC[:, :], triC_d[:, :], w=[triC])
    triW = kb.tile([128, 512], BF16)
    kb.dma("sp", triW[:, :], triW_d[:, :], w=[triW])
    cmask = [kb.tile([128, 512], BF16) for _ in range(17)]
    for m in range(17):
        kb.dma("sp", cmask[m][:, :], cmask_d[m], w=[cmask[m]])
    b1t = kb.tile([128, 2], F32)
    kb.dma("sp", b1t[:, :], b1[:, :], w=[b1t])

    KsE = kb.tile([72, S], BF16)
    KwE = kb.tile([72, S], BF16)
    KcE = kb.tile([72, NCC * 128], BF16)
    VsE = kb.tile([128, NKT * 65], BF16)
    VwE = kb.tile([128, NKT * 65], BF16)
    VcE = kb.tile([128, NCC * 193], BF16)
    VsE3 = VsE[:, :].rearrange("p (k c) -> p k c", c=65)
    VwE3 = VwE[:, :].rearrange("p (k c) -> p k c", c=65)
    VcE3 = VcE[:, :].rearrange("p (k c) -> p k c", c=193)
    kb.op("dve", lambda e: e.memset(KcE[:, :], 0.0), w=[KcE])
    kb.op("dve", lambda e: e.memset(VsE[:, :], 1.0), w=[VsE])
    kb.op("dve", lambda e: e.memset(VwE[:, :], 1.0), w=[VwE])
    kb.op("dve", lambda e: e.memset(VcE[:, :], 0.0), w=[VcE])
    kb.dma("sp", KsE[64:72, :], kaug[:, :], w=[KsE])
    kb.dma("sp", KwE[64:72, :], kaug[:, :], w=[KwE])
    kb.dma("sp", KcE[64:72, :], caug[:, :], r=[], w=[KcE])
    for cc in range(NCC):
        kb.dma("sp", VcE3[:, cc, 0:129], ovl_d[cc], w=[VcE])

    kcs = kb.tile([64, S], BF16)
    w1b = kb.tile([64, 32 * 128], BF16)
    w2b = kb.tile([128, 64], BF16)
    posf = kb.tile([64, 32], F32)
    posb = kb.tile([64, 32], BF16)
    biasc = kb.tile([128, 1], F32)
    zt = kb.tile([128, 512], F32)
    z2 = kb.tile([128, 512], F32)
    sg = kb.tile([128, 512], F32)
    Gb = kb.tile([128, NCC * 128], BF16)
    kb.op("dve", lambda e: e.memset(Gb[:, :], 0.0), w=[Gb])

    Qx = [kb.tile([72, 512], BF16) for _ in range(2)]
    gtl = [kb.tile([128, 12], F32) for _ in range(2)]
    Ftl = [kb.tile([128, 128], F32) for _ in range(2)]
    Et = Rot([kb.tile([128, 512], BF16) for _ in range(3)])
    nmT = kb.tile([128, 512], BF16)
    nm = kb.tile([128, 128], BF16)
    imp = kb.tile([128, 128], F32)
    wk = kb.tile([128, 128], F32)
    m8 = kb.tile([128, 16], F32)
    thr = kb.tile([128, 1], F32)
    zz = kb.tile([128, 12], F32)
    rz = kb.tile([128, 12], F32)
    wgt = kb.tile([128, 12], F32)
    att = kb.tile([128, 256], F32)
    tmp = kb.tile([128, 256], F32)
    attb = kb.tile([128, 256], BF16)
    aT = Rot([kb.tile([128, 128], BF16) for _ in range(4)])

    PSc = Rot([kb.psum([128, 512], F32) for _ in range(3)])
    pC = kb.psum([128, 1024], F32)
    pS = kb.psum([128, 512], F32)
    pW = kb.psum([128, 512], F32)
    pT = kb.psum([128, 128], BF16)
    pC3 = pC[:, :].rearrange("p (h c) -> p h c", c=256)
    pS3 = pS[:, :].rearrange("p (h c) -> p h c", c=128)
    pW3 = pW[:, :].rearrange("p (h c) -> p h c", c=128)

    def load_q(gl, T):
        q = Qx[T % 2]
        kb.dma("sp", q[0:64, :].rearrange("d (h q) -> d h q", h=4),
               qT[4 * gl:4 * gl + 4, :, T * 128:(T + 1) * 128].rearrange("h d q -> d h q"), w=[q])
        kb.dma("sp", q[64:72, :].rearrange("d (h q) -> d h q", h=4),
               qaug[4 * gl:4 * gl + 4, :, T * 128:(T + 1) * 128].rearrange("h d q -> d h q"), w=[q])
        kb.dma("sp", gtl[T % 2][:, :], gates[T * 128:(T + 1) * 128, 12 * gl:12 * gl + 12], w=[gtl[T % 2]])
        kb.dma("sp", Ftl[T % 2][:, :], F_d[T], w=[Ftl[T % 2]])

    for gl in range(2):
        kb.dma("sp", KsE[0:64, :], ksT[gl], w=[KsE])
        kb.dma("sp", KwE[0:64, :], kwT[gl], w=[KwE])
        for k0 in range(0, NKT, 16):
            k1 = min(NKT, k0 + 16)
            kb.dma("sp", VsE3[:, k0:k1, 0:64], vs[gl, k0 * 128:k1 * 128, :].rearrange("(k p) d -> p k d", p=128), w=[VsE])
            kb.dma("sp", VwE3[:, k0:k1, 0:64], vw[gl, k0 * 128:k1 * 128, :].rearrange("(k p) d -> p k d", p=128), w=[VwE])
        for which in range(2):
            kb.dma("sp", kcs[:, :], kcT[which, gl], w=[kcs])
            kb.dma("sp", w1b[:, :], w1[which], w=[w1b])
            kb.dma("sp", w2b[:, :], w2[which], w=[w2b])
            kb.dma("sp", posf[:, :], posT[which], w=[posf])
            kb.op("dve", lambda e: e.tensor_copy(out=posb[:, :], in_=posf[:, :]), r=[posf], w=[posb])
            pbias = PSc()
            for r in range(32):
                kb.op("pe", lambda e, r=r: e.matmul(pbias[:, 0:1], w1b[:, r * 128:(r + 1) * 128], posb[:, r:r + 1],
                                                    start=(r == 0), stop=(r == 31)), r=[w1b, posb], w=[pbias])
            kb.op("dve", lambda e: e.tensor_tensor(out=biasc[:, :], in0=pbias[:, 0:1], in1=b1t[:, which:which + 1], op=ALU.add),
                  r=[pbias, b1t], w=[biasc])
            ph = PSc()
            for r in range(32):
                kb.op("pe", lambda e, r=r: e.matmul(ph[:, 0:n_c], w1b[:, r * 128:(r + 1) * 128],
                                                    kcs[:, r:r + 16 * (n_c - 1) + 1:16],
                                                    start=(r == 0), stop=(r == 31)), r=[w1b, kcs], w=[ph])
            kb.op("act", lambda e: e.activation(out=zt[:, 0:n_c], in_=ph[:, 0:n_c], func=AF.Identity, bias=biasc[:, 0:1]),
                  r=[ph, biasc], w=[zt])
            kb.op("dve", lambda e: e.tensor_tensor(out=z2[:, 0:n_c], in0=zt[:, 0:n_c], in1=zt[:, 0:n_c], op=ALU.mult), r=[zt], w=[z2])
            kb.op("dve", lambda e: e.tensor_scalar(out=z2[:, 0:n_c], in0=z2[:, 0:n_c], scalar1=0.044715, scalar2=1.0,
                                                   op0=ALU.mult, op1=ALU.add), r=[z2], w=[z2])
            kb.op("dve", lambda e: e.tensor_tensor(out=z2[:, 0:n_c], in0=z2[:, 0:n_c], in1=zt[:, 0:n_c], op=ALU.mult), r=[z2, zt], w=[z2])
            kb.op("act", lambda e: e.activation(out=sg[:, 0:n_c], in_=z2[:, 0:n_c], func=AF.Sigmoid, scale=1.5957691216057308),
                  r=[z2], w=[sg])
            kb.op("dve", lambda e: e.tensor_tensor(out=Gb[:, 0:n_c], in0=zt[:, 0:n_c], in1=sg[:, 0:n_c], op=ALU.mult), r=[zt, sg], w=[Gb])
            if which == 0:
                pk = PSc()
                kb.op("pe", lambda e: e.matmul(pk[0:64, 0:n_c], w2b[:, :], Gb[:, 0:n_c], start=True, stop=True), r=[w2b, Gb], w=[pk])
                kb.op("act", lambda e: e.activation(out=KcE[0:64, 0:n_c], in_=pk[0:64, 0:n_c], func=AF.Copy), r=[pk], w=[KcE])
            else:
                for cc in range(NCC):
                    pv = PSc()
                    kb.op("pe", lambda e: e.matmul(pv[:, 0:64], Gb[:, cc * 128:(cc + 1) * 128], w2b[:, :], start=True, stop=True),
                          r=[Gb, w2b], w=[pv])
                    kb.op("act", lambda e: e.activation(out=VcE3[:, cc, 129:193], in_=pv[:, 0:64], func=AF.Copy), r=[pv], w=[VcE])

        load_q(gl, 0)
        for T in range(NQT):
            if T + 1 < NQT:
                load_q(gl, T + 1)
            Q = Qx[T % 2]
            gt = gtl[T % 2]
            Ft = Ftl[T % 2]
            items = []

            def mk_item(lhs_tile, lhs_ap, extra, vrhs, vtile, pacc, pacc_cols, width, first, last):
                st = {}

                def score():
                    ps = PSc()
                    st["ps"] = ps
                    kb.op("pe", lambda e: e.matmul(ps[:, :], lhs_ap, Q[0:72, :], start=True, stop=(len(extra) == 0)),
                          r=[lhs_tile, Q], w=[ps])
                    for xi, (xl_t, xl_ap, xr_t, xr_ap) in enumerate(extra):
                        kb.op("pe", lambda e: e.matmul(ps[:, :], xl_ap, xr_ap, start=False, stop=(xi == len(extra) - 1)),
                              r=[xl_t, xr_t], w=[ps])

                def ex():
                    et = Et()
                    st["et"] = et
                    ps = st["ps"]
                    kb.op("act", lambda e: e.activation(out=et[:, :], in_=ps[:, :], func=AF.Exp), r=[ps], w=[et])

                def pv():
                    et = st["et"]
                    for h in range(4):
                        st_h = first and (h == 0 or (pacc_cols == 256 and h == 2))
                        kb.op("pe", lambda e, h=h: e.matmul(pacc[:, pacc_cols * h:pacc_cols * h + width], et[:, h * 128:(h + 1) * 128],
                                                            vrhs, start=st_h, stop=last, skip_group_check=True), r=[et, vtile], w=[pacc])
                return {"score": score, "exp": ex, "pv": pv}

            ncc = min(NCC, (128 * T + 96) // 2048 + 1)
            for cc in range(ncc):
                partial = not (T >= 16 * cc + 17)
                extra = []
                if partial:
                    extra.append((ident, ident[:, :], cmask[T - 16 * cc], cmask[T - 16 * cc][:, :]))
                items.append(mk_item(KcE, KcE[0:72, cc * 128:(cc + 1) * 128], extra, VcE3[:, cc, :], VcE, pC, 256, 193,
                                     cc == 0, cc == ncc - 1))

            def topk():
                kb.op("dve", lambda e: e.tensor_scalar(out=zz[:, 0:4], in0=pC3[:, :, 128], scalar1=1e-30, scalar2=None, op0=ALU.max),
                      r=[pC], w=[zz])
                kb.op("dve", lambda e: e.reciprocal(out=rz[:, 0:4], in_=zz[:, 0:4]), r=[zz], w=[rz])
                kb.op("dve", lambda e: e.tensor_scalar(out=imp[:, :], in0=pC[:, 0:128], scalar1=rz[:, 0:1], scalar2=None, op0=ALU.mult),
                      r=[pC, rz], w=[imp])
                for h in range(1, 4):
                    kb.op("dve", lambda e, h=h: e.scalar_tensor_tensor(out=imp[:, :], in0=pC[:, 256 * h:256 * h + 128], scalar=rz[:, h:h + 1],
                                                                       in1=imp[:, :], op0=ALU.mult, op1=ALU.add), r=[pC, rz, imp], w=[imp])
                kb.op("dve", lambda e: e.tensor_tensor(out=imp[:, :], in0=imp[:, :], in1=Ft[:, :], op=ALU.add), r=[imp, Ft], w=[imp])
                kb.op("dve", lambda e: e.max(out=m8[:, 0:8], in_=imp[:, :]), r=[imp], w=[m8])
                kb.op("dve", lambda e: e.match_replace(out=wk[:, :], in_to_replace=m8[:, 0:8], in_values=imp[:, :], imm_value=-3.0e38),
                      r=[m8, imp], w=[wk])
                kb.op("dve", lambda e: e.max(out=m8[:, 8:16], in_=wk[:, :]), r=[wk], w=[m8])
                kb.op("dve", lambda e: e.tensor_scalar(out=thr[:, :], in0=m8[:, 15:16], scalar1=-1e29, scalar2=None, op0=ALU.max),
                      r=[m8], w=[thr])
                kb.op("dve", lambda e: e.tensor_scalar(out=nm[:, :], in0=imp[:, :], scalar1=thr[:, 0:1], scalar2=-1.0,
                                                       op0=ALU.is_ge, op1=ALU.add), r=[imp, thr], w=[nm])
            items[-1]["post"] = topk

            wl = [w_ for w_ in range(5) if T - 4 + w_ >= 0]
            for w_ in wl:
                kt = T - 4 + w_
                extra = []
                if w_ == 0:
                    extra.append((ident, ident[:, :], triW, triW[:, :]))
                if w_ == 4:
                    extra.append((ident, ident[:, :], triC, triC[:, :]))
                items.append(mk_item(KwE, KwE[0:72, kt * 128:(kt + 1) * 128], extra, VwE3[:, kt, :], VwE, pW, 128, 65,
                                     w_ == wl[0], w_ == 4))

            def mask_t():
                kb.op("pe", lambda e: e.transpose(out=pT[:, :], in_=nm[:, :], identity=ident[:, :]), r=[nm, ident], w=[pT])
                for h in range(4):
                    kb.op("act", lambda e, h=h: e.activation(out=nmT[:, h * 128:(h + 1) * 128], in_=pT[:, :], func=AF.Copy),
                          r=[pT], w=[nmT])
            first_slc = len(items)
            for kt in range(T + 1):
                extra = [(Ex, Ex[:, kt * 128:(kt + 1) * 128], nmT, nmT[:, :])]
                if kt == T:
                    extra.append((ident, ident[:, :], triC, triC[:, :]))
                items.append(mk_item(KsE, KsE[0:72, kt * 128:(kt + 1) * 128], extra, VsE3[:, kt, :], VsE, pS, 128, 65,
                                     kt == 0, kt == T))
            items[first_slc]["pre"] = mask_t

            n = len(items)
            if "pre" in items[0]:
                items[0]["pre"]()
            items[0]["score"]()
            for i in range(n):
                if i + 1 < n:
                    if "pre" in items[i + 1]:
                        items[i + 1]["pre"]()
                    items[i + 1]["score"]()
                items[i]["exp"]()
                items[i]["pv"]()
                if "post" in items[i]:
                    items[i]["post"]()

            kb.op("dve", lambda e: e.tensor_scalar(out=zz[:, 4:8], in0=pS3[:, :, 64], scalar1=1e-30, scalar2=None, op0=ALU.max),
                  r=[pS], w=[zz])
            kb.op("dve", lambda e: e.tensor_scalar(out=zz[:, 8:12], in0=pW3[:, :, 64], scalar1=1e-30, scalar2=None, op0=ALU.max),
                  r=[pW], w=[zz])
            kb.op("dve", lambda e: e.reciprocal(out=rz[:, 4:12], in_=zz[:, 4:12]), r=[zz], w=[rz])
            gt3 = gt[:, :].rearrange("p (h b) -> p h b", b=3)
            for br in range(3):
                kb.op("dve", lambda e, br=br: e.tensor_tensor(out=wgt[:, 4 * br:4 * br + 4], in0=gt3[:, :, br], in1=rz[:, 4 * br:4 * br + 4],
                                                              op=ALU.mult), r=[gt, rz], w=[wgt])
            att3 = att[:, :].rearrange("p (h c) -> p h c", c=64)
            tmp3 = tmp[:, :].rearrange("p (h c) -> p h c", c=64)

            def wb_(br):
                return wgt[:, 4 * br:4 * br + 4].unsqueeze(2).to_broadcast([128, 4, 64])
            kb.op("dve", lambda e: e.tensor_tensor(out=att3, in0=pC3[:, :, 129:193], in1=wb_(0), op=ALU.mult), r=[pC, wgt], w=[att])
            kb.op("dve", lambda e: e.tensor_tensor(out=tmp3, in0=pS3[:, :, 0:64], in1=wb_(1), op=ALU.mult), r=[pS, wgt], w=[tmp])
            kb.op("dve", lambda e: e.tensor_tensor(out=att[:, :], in0=att[:, :], in1=tmp[:, :], op=ALU.add), r=[att, tmp], w=[att])
            kb.op("dve", lambda e: e.tensor_tensor(out=tmp3, in0=pW3[:, :, 0:64], in1=wb_(2), op=ALU.mult), r=[pW, wgt], w=[tmp])
            kb.op("dve", lambda e: e.tensor_tensor(out=attb[:, :], in0=att[:, :], in1=tmp[:, :], op=ALU.add), r=[att, tmp], w=[attb])
            for hf in range(2):
                kb.op("pe", lambda e: e.transpose(out=pT[:, :], in_=attb[:, hf * 128:(hf + 1) * 128], identity=ident[:, :]),
                      r=[attb, ident], w=[pT])
                a = aT()
                kb.op("act", lambda e: e.activation(out=a[:, :], in_=pT[:, :], func=AF.Copy), r=[pT], w=[a])
                kb.dma("pool", attnT[gl * 256 + hf * 128:gl * 256 + hf * 128 + 128, T * 128:(T + 1) * 128], a[:, :], r=[a])
    kb.finish()
    return nc


def build_D(S):
    TOK = S // 2
    NST = TOK // 512
    nc = bass.Bass("TRN2", target_bir_lowering=False)
    dt_in = lambda name, shape, dt: nc.dram_tensor(name, list(shape), dt, kind="ExternalInput").ap()
    attnT = dt_in("attnT", [8, 128, TOK], BF16)
    cmT = dt_in("cmT", [8, 128, TOK], F32)
    gaT = dt_in("gaT", [8, 128, TOK], F32)
    xT = dt_in("xT", [8, 128, TOK], F32)
    wo_d = dt_in("wo", [8, 128, D], BF16)
    wfi_d = dt_in("wfi", [8, 128, 2 * DFF], BF16)
    wfo_d = dt_in("wfo", [22, 128, D], BF16)
    lnp_d = dt_in("lnp", [128, 32], F32)
    ones_d = dt_in("ones", [128, 128], BF16)
    outT = nc.dram_tensor("outT", [8, 128, TOK], F32, kind="ExternalOutput").ap()
    kb = KB(nc)

    wo = [kb.tile([128, D], BF16) for _ in range(8)]
    for k in range(8):
        kb.dma("sp", wo[k][:, :], wo_d[k], w=[wo[k]])
    wfo = [kb.tile([128, D], BF16) for _ in range(22)]
    for j in range(22):
        kb.dma("sp", wfo[j][:, :], wfo_d[j], w=[wfo[j]])
    lnp = kb.tile([128, 32], F32)
    kb.dma("sp", lnp[:, :], lnp_d[:, :], w=[lnp])
    ones = kb.tile([128, 128], BF16)
    kb.dma("sp", ones[:, :], ones_d[:, :], w=[ones])

    wfi = [[kb.tile([128, 512], BF16) for _ in range(8)] for _ in range(2)]
    at = [kb.tile([128, 512], BF16) for _ in range(8)]
    mg = [kb.tile([128, 512], BF16) for _ in range(8)]
    y = [kb.tile([128, 512], F32) for _ in range(8)]
    x1 = [kb.tile([128, 512], F32) for _ in range(8)]
    x1b = [kb.tile([128, 512], BF16) for _ in range(8)]
    hb = [kb.tile([128, 512], BF16) for _ in range(22)]
    f32r = Rot([kb.tile([128, 512], F32) for _ in range(8)])
    bfr = Rot([kb.tile([128, 512], BF16) for _ in range(4)])
    mean = kb.tile([128, 512], F32)
    rstd = kb.tile([128, 512], F32)
    msq = kb.tile([128, 512], F32)
    PS = Rot([kb.psum([128, 512], F32) for _ in range(6)])
    pS1 = kb.psum([128, 512], F32)
    pS2 = kb.psum([128, 512], F32)

    def load_wfi(idx):
        jj = idx % 11
        buf = wfi[idx % 2]
        for k in range(8):
            kb.dma("sp", buf[k][:, 0:256], wfi_d[k, :, 256 * jj:256 * jj + 256], w=[buf[k]])
            kb.dma("sp", buf[k][:, 256:512], wfi_d[k, :, DFF + 256 * jj:DFF + 256 * jj + 256], w=[buf[k]])

    def layernorm(src, goff, boff, dst, dstb, store=None):
        for k in range(8):
            yb = bfr()
            kb.op("act", lambda e: e.activation(out=yb[:, :], in_=src[k][:, :], func=AF.Copy), r=[src[k]], w=[yb])
            kb.op("pe", lambda e: e.matmul(pS1[:, :], ones[:, :], yb[:, :], start=(k == 0), stop=(k == 7)), r=[ones, yb], w=[pS1])
            ysq = bfr()
            kb.op("act", lambda e: e.activation(out=ysq[:, :], in_=src[k][:, :], func=AF.Square), r=[src[k]], w=[ysq])
            kb.op("pe", lambda e: e.matmul(pS2[:, :], ones[:, :], ysq[:, :], start=(k == 0), stop=(k == 7)), r=[ones, ysq], w=[pS2])
        kb.op("dve", lambda e: e.tensor_scalar(out=mean[:, :], in0=pS1[:, :], scalar1=1.0 / D, scalar2=None, op0=ALU.mult), r=[pS1], w=[mean])
        kb.op("dve", lambda e: e.tensor_tensor(out=msq[:, :], in0=mean[:, :], in1=mean[:, :], op=ALU.mult), r=[mean], w=[msq])
        kb.op("dve", lambda e: e.scalar_tensor_tensor(out=rstd[:, :], in0=pS2[:, :], scalar=1.0 / D, in1=msq[:, :],
                                                      op0=ALU.mult, op1=ALU.subtract), r=[pS2, msq], w=[rstd])
        kb.op("dve", lambda e: e.tensor_scalar(out=rstd[:, :], in0=rstd[:, :], scalar1=LN_EPS, scalar2=None, op0=ALU.add),
              r=[rstd], w=[rstd])
        kb.op("act", lambda e: e.activation(out=msq[:, :], in_=rstd[:, :], func=AF.Sqrt), r=[rstd], w=[msq])
        kb.op("dve", lambda e: e.reciprocal(out=rstd[:, :], in_=msq[:, :]), r=[msq], w=[rstd])
        for k in range(8):
            t = f32r()
            kb.op("dve", lambda e: e.tensor_tensor(out=t[:, :], in0=src[k][:, :], in1=mean[:, :], op=ALU.subtract), r=[src[k], mean], w=[t])
            kb.op("dve", lambda e: e.tensor_tensor(out=t[:, :], in0=t[:, :], in1=rstd[:, :], op=ALU.mult), r=[t, rstd], w=[t])
            kb.op("dve", lambda e: e.tensor_scalar(out=dst[k][:, :], in0=t[:, :], scalar1=lnp[:, goff + k:goff + k + 1],
                                                   scalar2=lnp[:, boff + k:boff + k + 1], op0=ALU.mult, op1=ALU.add),
                  r=[t, lnp], w=[dst[k]])
            if dstb is not None:
                kb.op("act", lambda e: e.activation(out=dstb[k][:, :], in_=dst[k][:, :], func=AF.Copy), r=[dst[k]], w=[dstb[k]])
            if store is not None:
                store(k)

    widx = 0
    load_wfi(0)
    for st in range(NST):
        t0 = st * 512
        for k in range(8):
            kb.dma("sp", at[k][:, :], attnT[k, :, t0:t0 + 512], w=[at[k]])
            ga = f32r()
            kb.dma("sp", ga[:, :], gaT[k, :, t0:t0 + 512], w=[ga])
            cm = f32r()
            kb.dma("sp", cm[:, :], cmT[k, :, t0:t0 + 512], w=[cm])
            kb.op("dve", lambda e: e.tensor_tensor(out=ga[:, :], in0=ga[:, :], in1=at[k][:, :], op=ALU.mult), r=[ga, at[k]], w=[ga])
            kb.op("dve", lambda e: e.tensor_tensor(out=mg[k][:, :], in0=ga[:, :], in1=cm[:, :], op=ALU.add), r=[ga, cm], w=[mg[k]])
        for i in range(8):
            p = PS()
            for k in range(8):
                kb.op("pe", lambda e, k=k: e.matmul(p[:, :], wo[k][:, 128 * i:128 * i + 128], mg[k][:, :], start=(k == 0), stop=(k == 7)),
                      r=[wo[k], mg[k]], w=[p])
            xr = f32r()
            kb.dma("sp", xr[:, :], xT[i, :, t0:t0 + 512], w=[xr])
            kb.op("dve", lambda e: e.scalar_tensor_tensor(out=y[i][:, :], in0=xr[:, :], scalar=ALPHA, in1=p[:, :],
                                                          op0=ALU.mult, op1=ALU.add), r=[xr, p], w=[y[i]])
        layernorm(y, 0, 8, x1, x1b)
        for jj in range(11):
            buf = wfi[widx % 2]
            if not (st == NST - 1 and jj == 10):
                load_wfi(widx + 1)
            for jl in range(2):
                j = 2 * jj + jl
                pa = PS()
                for k in range(8):
                    kb.op("pe", lambda e, k=k: e.matmul(pa[:, :], buf[k][:, 128 * jl:128 * jl + 128], x1b[k][:, :],
                                                        start=(k == 0), stop=(k == 7)), r=[buf[k], x1b[k]], w=[pa])
                pu = PS()
                for k in range(8):
                    kb.op("pe", lambda e, k=k: e.matmul(pu[:, :], buf[k][:, 256 + 128 * jl:256 + 128 * jl + 128], x1b[k][:, :],
                                                        start=(k == 0), stop=(k == 7)), r=[buf[k], x1b[k]], w=[pu])
                sa = f32r()
                kb.op("act", lambda e: e.activation(out=sa[:, :], in_=pa[:, :], func=AF.Silu), r=[pa], w=[sa])
                kb.op("dve", lambda e: e.tensor_tensor(out=hb[j][:, :], in0=sa[:, :], in1=pu[:, :], op=ALU.mult), r=[sa, pu], w=[hb[j]])
            widx += 1
        for i in range(8):
            p = PS()
            for j in range(22):
                kb.op("pe", lambda e, j=j: e.matmul(p[:, :], wfo[j][:, 128 * i:128 * i + 128], hb[j][:, :], start=(j == 0), stop=(j == 21)),
                      r=[wfo[j], hb[j]], w=[p])
            kb.op("dve", lambda e: e.scalar_tensor_tensor(out=y[i][:, :], in0=x1[i][:, :], scalar=ALPHA, in1=p[:, :],
                                                          op0=ALU.mult, op1=ALU.add), r=[x1[i], p], w=[y[i]])

        def store(k):
            kb.dma("pool", outT[k, :, t0:t0 + 512], x1[k][:, :], r=[x1[k]])
        layernorm(y, 16, 24, x1, None, store=store)
    kb.finish()
    return nc


def _bf(a):
    return np.ascontiguousarray(np.asarray(a, dtype=np.float32).astype(NPBF))


def _split3(v):
    v = np.asarray(v, dtype=np.float64)
    a = v.astype(np.float32).astype(NPBF).astype(np.float64)
    b = (v - a).astype(np.float32).astype(NPBF).astype(np.float64)
    c = (v - a - b).astype(np.float32).astype(NPBF).astype(np.float64)
    return a, b, c


def make_consts(S):
    NQT = S // 128
    NKT = S // 128
    n_c = S // 16 - 1
    NCC = (n_c + 127) // 128
    n_sel = S // 64
    C = {}
    C["ident"] = _bf(np.eye(128))
    C["ones"] = _bf(np.ones((128, 128)))
    Ex = np.zeros((128, NKT, 128), np.float32)
    for kt in range(NKT):
        for m in range(128):
            j = 2 * kt + m // 64
            if j < 128:
                Ex[j, kt, m] = NEGM
    C["Ex"] = _bf(Ex.reshape(128, NKT * 128))
    kl = np.arange(128)[:, None]
    ql = np.arange(128)[None, :]
    C["triC"] = _bf(np.tile(np.where(kl > ql, -NEGM, 0.0), (1, 4)))
    C["triW"] = _bf(np.tile(np.where(kl <= ql, -NEGM, 0.0), (1, 4)))
    cm = np.zeros((17, 128, 128), np.float32)
    for m in range(17):
        cm[m] = np.where(16 * kl + 31 > 128 * m + ql, -NEGM, 0.0)
    C["cmask"] = _bf(np.tile(cm, (1, 1, 4)))
    t = np.arange(S)
    cur = t // 64
    jj = np.arange(128)[None, :]
    Ft = np.zeros((S, 128), np.float32)
    forced = (jj == 0) | (jj == cur[:, None]) | (jj == cur[:, None] - 1)
    Ft[forced] = 1e4
    Ft[(jj > cur[:, None]) | (jj >= n_sel)] = -1e30
    C["Ftab"] = np.ascontiguousarray(Ft.reshape(NQT, 128, 128))
    ci = np.arange(NCC * 128)[:, None]
    sj = np.arange(128)[None, :]
    ov = ((ci * 16 <= sj * 64 + 63) & (ci * 16 + 31 >= sj * 64) & (ci < n_c) & (sj < n_sel)).astype(np.float32)
    ovl = np.concatenate([ov, np.ones((NCC * 128, 1), np.float32)], axis=1)
    C["ovl"] = _bf(ovl.reshape(NCC, 128, 129))
    h = np.arange(1, NH + 1, dtype=np.float64)
    slopes32 = (np.float32(2.0) ** (np.float32(-8.0) * h.astype(np.float32) / np.float32(NH))).astype(np.float64)
    s_hi = slopes32.astype(np.float32).astype(NPBF).astype(np.float64)
    s_lo = (slopes32 - s_hi).astype(np.float32).astype(NPBF).astype(np.float64)
    s_a = s_hi + s_lo
    qaug = np.zeros((NH, 8, S), np.float64)
    u = -s_a[:, None] * t[None, :].astype(np.float64)
    u1, u2, u3 = _split3(u)
    qaug[:, 0], qaug[:, 1], qaug[:, 2] = u1, u2, u3
    qaug[:, 3] = s_hi[:, None]
    qaug[:, 4] = s_lo[:, None]
    qaug[:, 5] = s_hi[:, None]
    qaug[:, 6] = s_lo[:, None]
    qaug[:, 7] = 1.0
    C["qaug"] = _bf(qaug)
    kaug = np.zeros((8, S), np.float64)
    kaug[0:3] = 1.0
    kaug[3] = kaug[4] = (t // 128) * 128
    kaug[5] = kaug[6] = t % 128
    C["kaug"] = _bf(kaug)
    c = np.arange(NCC * 128)
    ce = 16 * c + 31
    caug = np.zeros((8, NCC * 128), np.float64)
    caug[0:3] = 1.0
    caug[3] = caug[4] = (ce // 128) * 128
    caug[5] = caug[6] = ce % 128
    caug[7] = np.where(c < n_c, 0.0, -NEGM)
    caug[3:7, c >= n_c] = 0.0
    C["caug"] = _bf(caug)
    return C


_PROG = {}


def _prog(kind, S, *a):
    key = (kind, S) + a
    if key not in _PROG:
        _PROG[key] = {"W": build_W, "A": build_A, "C": build_C, "D": build_D}[kind](*((a if kind == "W" else (S,))))
    return _PROG[key]


def _run(nc, in_maps):
    res = run_bass_kernel_spmd(nc, in_maps, core_ids=list(range(NCORES)))
    return res.results


def cast_weights(arrs):
    flat = np.concatenate([np.ascontiguousarray(a, dtype=np.float32).reshape(-1) for a in arrs])
    n = flat.size
    per = -(-n // (NCORES * 128 * 16)) * 16
    pad = np.zeros(per * 128 * NCORES, np.float32)
    pad[:n] = flat
    parts = pad.reshape(NCORES, 128, per)
    nc = _prog("W", 0, per)
    res = _run(nc, [{"src": parts[c]} for c in range(NCORES)])
    out = np.concatenate([np.asarray(res[c]["dst"]).reshape(-1) for c in range(NCORES)])[:n]
    outs = []
    o = 0
    for a in arrs:
        outs.append(out[o:o + a.size].reshape(a.shape))
        o += a.size
    return outs


def kernel(x, w_in, conv_w, w_conv_out, cmp_pos, cmp_w1, cmp_b1, cmp_w2, w_o,
           ln1_g, ln1_b, w_ffn_in, w_ffn_out, ln2_g, ln2_b):
    x = np.asarray(x, dtype=np.float32)
    B, S, _ = x.shape
    TOK = S // 2
    C = make_consts(S)
    wl = [np.asarray(a, dtype=np.float32) for a in (w_in, w_conv_out, w_o, w_ffn_in, w_ffn_out, cmp_w1, cmp_w2)]
    w_in_b, wco_b, wo_b, wfi_b, wfo_b, w1_b, w2_b = cast_weights(wl)
    conv_w = np.asarray(conv_w, np.float32)
    cmp_pos = np.asarray(cmp_pos, np.float32)
    cmp_b1 = np.asarray(cmp_b1, np.float32)
    lnl = [np.asarray(a, np.float32) for a in (ln1_g, ln1_b, ln2_g, ln2_b)]

    xT = np.ascontiguousarray(x.transpose(0, 2, 1))
    ncA, ncC, ncD = _prog("A", S), _prog("C", S), _prog("D", S)
    for l in range(DEPTH):
        in_maps = []
        for c in range(NCORES):
            b, hf = c // 2, c % 2
            xin = np.zeros((D, TOK + 2), np.float32)
            xin[:, 2:] = xT[b, :, hf * TOK:(hf + 1) * TOK]
            if hf == 1:
                xin[:, 0:2] = xT[b, :, TOK - 2:TOK]
            in_maps.append({
                "xT": xin.reshape(8, 128, TOK + 2),
                "w_in": w_in_b[l].reshape(8, 128, NIN),
                "wco": wco_b[l].reshape(4, 128, D),
                "convw": np.ascontiguousarray(conv_w[l].reshape(3, 4, 128).transpose(2, 1, 0).reshape(128, 12)),
            })
        rA = _run(ncA, in_maps)
        cat = lambda name, ax: [np.concatenate([np.asarray(rA[2 * b][name]), np.asarray(rA[2 * b + 1][name])], axis=ax) for b in range(B)]
        qT_f = cat("qT", 1)
        kT_f = cat("kT", 2)
        vt_f = cat("vtok", 0)
        g_f = cat("gates", 0)
        in_maps = []
        for c in range(NCORES):
            b, p = c // 2, c % 2
            kt = kT_f[b].reshape(4, 4, 64, S)
            vt = vt_f[b].reshape(S, 2, 4, 64)
            m = {
                "qT": np.ascontiguousarray(qT_f[b].reshape(16, 64, S)[8 * p:8 * p + 8]),
                "qaug": np.ascontiguousarray(C["qaug"][8 * p:8 * p + 8]),
                "ksT": np.ascontiguousarray(kt[0, 2 * p:2 * p + 2]),
                "kwT": np.ascontiguousarray(kt[1, 2 * p:2 * p + 2]),
                "kcT": np.ascontiguousarray(kt[2:4, 2 * p:2 * p + 2]),
                "kaug": C["kaug"], "caug": C["caug"],
                "vs": np.ascontiguousarray(vt[:, 0, 2 * p:2 * p + 2].transpose(1, 0, 2)),
                "vw": np.ascontiguousarray(vt[:, 1, 2 * p:2 * p + 2].transpose(1, 0, 2)),
                "gates": np.ascontiguousarray(g_f[b][:, 24 * p:24 * p + 24]),
                "w1": np.ascontiguousarray(w1_b[l].reshape(2, 32, 64, 128).transpose(0, 2, 1, 3).reshape(2, 64, 32 * 128)),
                "w2": w2_b[l],
                "b1": np.ascontiguousarray(cmp_b1[l].T),
                "posT": np.ascontiguousarray(cmp_pos[l].transpose(0, 2, 1)),
                "ident": C["ident"], "Ex": C["Ex"], "triC": C["triC"], "triW": C["triW"],
                "cmask": C["cmask"], "Ftab": C["Ftab"], "ovl": C["ovl"],
            }
            in_maps.append(m)
        rC = _run(ncC, in_maps)
        in_maps = []
        lnp = np.concatenate([a[l].reshape(8, 128).T for a in lnl], axis=1)
        for c in range(NCORES):
            b, hf = c // 2, c % 2
            sl = slice(hf * TOK, (hf + 1) * TOK)
            attn_full = np.concatenate([np.asarray(rC[2 * b]["attnT"]), np.asarray(rC[2 * b + 1]["attnT"])], axis=0)
            in_maps.append({
                "attnT": np.ascontiguousarray(attn_full[:, sl]).reshape(8, 128, TOK),
                "cmT": np.asarray(rA[c]["cmT"]).reshape(8, 128, TOK),
                "gaT": np.asarray(rA[c]["gaT"]).reshape(8, 128, TOK),
                "xT": np.ascontiguousarray(xT[b][:, sl]).reshape(8, 128, TOK),
                "wo": wo_b[l].reshape(8, 128, D),
                "wfi": wfi_b[l].reshape(8, 128, 2 * DFF),
                "wfo": wfo_b[l].reshape(22, 128, D),
                "lnp": np.ascontiguousarray(lnp),
                "ones": C["ones"],
            })
        rD = _run(ncD, in_maps)
        xT = np.stack([np.concatenate([np.asarray(rD[2 * b]["outT"]).reshape(D, TOK),
                                       np.asarray(rD[2 * b + 1]["outT"]).reshape(D, TOK)], axis=1) for b in range(B)])
    return np.ascontiguousarray(xT.transpose(0, 2, 1)).astype(np.float32)
```
